# Optimizing a Trainium2 kernel written in Bass

```python
import jax, jax.numpy as jnp
from jax import lax
import numpy as np

D_MODEL = 1024
BATCH = 4
SEQ = 8192
DEPTH = 1

HEAD_DIM = 64
ATTN_WIDTH = 3 * D_MODEL // 4
N_ATTN_HEADS = ATTN_WIDTH // HEAD_DIM
CONV_WIDTH = D_MODEL - ATTN_WIDTH
MIX_WIDTH = ATTN_WIDTH + CONV_WIDTH
IN_WIDTH = 4 * ATTN_WIDTH + 4 * CONV_WIDTH
CONV_K = 3
DILATED_PATTERNS = ((128, 1), (512, 4), (2048, 16))
BLOCK = 128
ROPE_THETA = 10000.0
NORM_EPS = 1e-6

kernel_name = "hybrid_dilated_attn_shortconv_layer"


def rmsnorm(x, g):
    xf = x.astype(jnp.float32)
    y = xf * lax.rsqrt(jnp.mean(xf * xf, axis=-1, keepdims=True) + NORM_EPS)
    return (y * g.astype(jnp.float32)).astype(x.dtype)


def rope(t, pos):
    half = t.shape[-1] // 2
    inv_freq = ROPE_THETA ** (-jnp.arange(half, dtype=jnp.float32) * 2.0 / t.shape[-1])
    ang = pos.astype(jnp.float32)[:, None] * inv_freq[None, :]
    cos = jnp.cos(ang)[None, :, None, :]
    sin = jnp.sin(ang)[None, :, None, :]
    tf = t.astype(jnp.float32)
    t1, t2 = tf[..., :half], tf[..., half:]
    out = jnp.concatenate([t1 * cos - t2 * sin, t2 * cos + t1 * sin], axis=-1)
    return out.astype(t.dtype)


def banded_window_attn(q, k, v, window_keys):
    N, L, H, hd = q.shape
    Lp = -(-L // BLOCK) * BLOCK
    padw = ((0, 0), (0, Lp - L), (0, 0), (0, 0))
    q, k, v = jnp.pad(q, padw), jnp.pad(k, padw), jnp.pad(v, padw)
    nb = Lp // BLOCK
    qb = q.reshape(N, nb, BLOCK, H, hd)
    kb = k.reshape(N, nb, BLOCK, H, hd)
    vb = v.reshape(N, nb, BLOCK, H, hd)
    shift = ((0, 0), (1, 0), (0, 0), (0, 0), (0, 0))
    kk = jnp.concatenate([jnp.pad(kb, shift)[:, :-1], kb], axis=2)
    vv = jnp.concatenate([jnp.pad(vb, shift)[:, :-1], vb], axis=2)
    s = jnp.einsum('nbqhd,nbkhd->nbhqk', qb, kk).astype(jnp.float32) * (hd ** -0.5)
    qi = jnp.arange(BLOCK)[:, None]
    kj = jnp.arange(2 * BLOCK)[None, :]
    dist = qi + BLOCK - kj
    key_abs = jnp.arange(nb)[:, None, None] * BLOCK + kj[None] - BLOCK
    valid = (dist >= 0)[None] & (dist <= window_keys)[None] & (key_abs >= 0)
    s = jnp.where(valid[None, :, None], s, -jnp.inf)
    m = jnp.max(s, axis=-1, keepdims=True)
    p = jnp.exp(s - m)
    den = jnp.sum(p, axis=-1, keepdims=True)
    lse = (m + jnp.log(den))[..., 0]
    o = jnp.einsum('nbhqk,nbkhd->nbqhd', (p / den).astype(v.dtype), vv)
    o = o.reshape(N, Lp, H, hd)[:, :L]
    lse = lse.transpose(0, 1, 3, 2).reshape(N, Lp, H)[:, :L]
    return o, lse


def dilated_window_attn(q, k, v, window, dilation):
    B, S, H, hd = q.shape
    L = S // dilation

    def to_res(t):
        return t.reshape(B, L, dilation, H, hd).transpose(0, 2, 1, 3, 4).reshape(B * dilation, L, H, hd)

    o, lse = banded_window_attn(to_res(q), to_res(k), to_res(v), window // dilation)
    o = o.reshape(B, dilation, L, H, hd).transpose(0, 2, 1, 3, 4).reshape(B, S, H, hd)
    lse = lse.reshape(B, dilation, L, H).transpose(0, 2, 1, 3).reshape(B, S, H)
    return o, lse


def causal_depthwise_conv(u, w):
    S = u.shape[1]
    up = jnp.pad(u, ((0, 0), (CONV_K - 1, 0), (0, 0)))
    y = up[:, 0:S] * w[0]
    for j in range(1, CONV_K):
        y = y + up[:, j:j + S] * w[j]
    return y


def setup_inputs(seed: int = 0) -> dict:
    key = jax.random.key(seed)
    ks = jax.random.split(key, 6)
    x = jax.random.normal(ks[0], (BATCH, SEQ, D_MODEL), jnp.float32)
    norm_pre_g = 1.0 + 0.01 * jax.random.normal(ks[1], (D_MODEL,), jnp.float32)
    w_in = jax.random.normal(ks[2], (D_MODEL, IN_WIDTH), jnp.float32) * D_MODEL ** -0.5
    conv_w = jax.random.normal(ks[3], (CONV_K, CONV_WIDTH), jnp.float32) * CONV_K ** -0.5
    w_out = jax.random.normal(ks[4], (MIX_WIDTH, D_MODEL), jnp.float32) * MIX_WIDTH ** -0.5
    norm_post_g = 1.0 + 0.01 * jax.random.normal(ks[5], (D_MODEL,), jnp.float32)
    return {"x": x, "norm_pre_g": norm_pre_g, "w_in": w_in, "conv_w": conv_w,
            "w_out": w_out, "norm_post_g": norm_post_g}


def reference(x, norm_pre_g, w_in, conv_w, w_out, norm_post_g):
    B, S, _ = x.shape
    pos = jnp.arange(S)
    for _layer in range(DEPTH):
        h = rmsnorm(x, norm_pre_g)
        z = jnp.einsum('bsd,de->bse', h, w_in)
        A, C = ATTN_WIDTH, CONV_WIDTH
        cuts = np.cumsum([A, A, A, A, C, C, C])
        q, k, v, g_attn, c_h, c_b, c_c, g_conv = jnp.split(z, cuts, axis=-1)

        q = rope(q.reshape(B, S, N_ATTN_HEADS, HEAD_DIM), pos)
        k = rope(k.reshape(B, S, N_ATTN_HEADS, HEAD_DIM), pos)
        v = v.reshape(B, S, N_ATTN_HEADS, HEAD_DIM)
        outs, lses = [], []
        for window, dilation in DILATED_PATTERNS:
            o_i, lse_i = dilated_window_attn(q, k, v, window, dilation)
            outs.append(o_i)
            lses.append(lse_i)
        mix_w = jax.nn.softmax(jnp.stack(lses, axis=0), axis=0)
        o = jnp.sum(mix_w[..., None] * jnp.stack(outs, axis=0).astype(jnp.float32), axis=0)
        attn_out = o.astype(x.dtype).reshape(B, S, ATTN_WIDTH) * jax.nn.silu(g_attn)

        conv_out = c_b * causal_depthwise_conv(c_c * c_h, conv_w.astype(x.dtype))
        conv_out = conv_out * jax.nn.silu(g_conv)

        mixed = jnp.concatenate([attn_out, conv_out], axis=-1)
        y = jnp.einsum('bse,ed->bsd', mixed, w_out)
        x = x + rmsnorm(y, norm_post_g)
    return x
```

```python
from contextlib import ExitStack

import numpy as np
import ml_dtypes

import concourse.bass as bass
import concourse.mybir as mybir
from concourse.bass_utils import run_bass_kernel_spmd

F32 = mybir.dt.float32
BF16 = mybir.dt.bfloat16
ALU = mybir.AluOpType
AF = mybir.ActivationFunctionType

ENGS = ("pe", "act", "dve", "pool", "sp")
NCORES = 8
D = 1024
SEQ = 8192
BATCH = 4
OWN = 2048
UT = 4096
NEG = -30000.0
EPS = 1e-6


class Prog:
    def __init__(self):
        self.ops = {e: [] for e in ENGS}
        self.cnt = {e: 0 for e in ENGS}
        self.seen = {e: {} for e in ENGS}
        self.bufs = {}
        self.sem = {}
        self.dma_cnt = {}

    def _wait(self, eng, ev):
        kind, key, val = ev
        if kind == "eng" and key == eng and eng == "pe":
            return
        k = (kind, key)
        if self.seen[eng].get(k, 0) >= val:
            return
        self.seen[eng][k] = val
        semh = self.sem[key]
        self.ops[eng].append(lambda h, semh=semh, val=val: h.wait_ge(semh, val))

    def _deps(self, eng, reads, writes):
        for b in reads:
            st = self.bufs.get(b)
            if st and st["w"]:
                self._wait(eng, st["w"])
        for b in writes:
            st = self.bufs.get(b)
            if st:
                if st["w"]:
                    self._wait(eng, st["w"])
                for r in st["r"]:
                    self._wait(eng, r)

    def _upd(self, ev, reads, writes):
        for b in reads:
            st = self.bufs.setdefault(b, {"w": None, "r": []})
            st["r"].append(ev)
        for b in writes:
            self.bufs[b] = {"w": ev, "r": []}

    def op(self, eng, fn, reads=(), writes=(), signal=True):
        self._deps(eng, reads, writes)
        if signal:
            self.cnt[eng] += 1
            ev = ("eng", eng, self.cnt[eng])
            semh = self.sem[eng]
            self.ops[eng].append(lambda h, fn=fn, semh=semh: fn(h).then_inc(semh, 1))
        else:
            ev = ("eng", eng, self.cnt[eng] + 1)
            self.ops[eng].append(lambda h, fn=fn: fn(h))
        self._upd(ev, reads, writes)

    def dma(self, q, semname, out, in_, reads=(), writes=()):
        self._deps(q, reads, writes)
        self.dma_cnt[semname] = self.dma_cnt.get(semname, 0) + 16
        val = self.dma_cnt[semname]
        ev = ("dma", semname, val)
        semh = self.sem[semname]
        self.ops[q].append(lambda h, out=out, in_=in_, semh=semh: h.dma_start(out, in_).then_inc(semh, 16))
        self._upd(ev, reads, writes)

    def wait_all_dma(self, eng, semnames):
        for s in semnames:
            if s in self.dma_cnt:
                self._wait(eng, ("dma", s, self.dma_cnt[s]))

    def run(self, block):
        m = {"pe": block.tensor, "act": block.scalar, "dve": block.vector, "pool": block.gpsimd, "sp": block.sync}
        for e in ENGS:
            lst = self.ops[e]

            def body(h, lst=lst):
                for f in lst:
                    f(h)

            m[e](body)


def tk(name, lo, hi, g=512):
    return [(name, j) for j in range(lo // g, (hi - 1) // g + 1)]


def build_nc(debug=False):
    nc = bass.Bass("TRN2", target_bir_lowering=False)
    dbg = {}
    xT_d = nc.dram_tensor("xT", [2, D, UT], F32, kind="ExternalInput").ap()
    xo_d = nc.dram_tensor("xown", [2, OWN, D], F32, kind="ExternalInput").ap()
    tab_d = nc.dram_tensor("tabs", [2, 128, UT], F32, kind="ExternalInput").ap()
    w_d = nc.dram_tensor("wr", [D, 4096], F32, kind="ExternalInput").ap()
    wo_d = nc.dram_tensor("wout", [D, D], F32, kind="ExternalInput").ap()
    gp_d = nc.dram_tensor("gpost", [128, D], F32, kind="ExternalInput").ap()
    cs_d = nc.dram_tensor("consts", [128, 16], F32, kind="ExternalInput").ap()
    mk_d = nc.dram_tensor("masks", [2, 3, 128, 512], F32, kind="ExternalInput").ap()
    out_d = nc.dram_tensor("out", [2, OWN, D], F32, kind="ExternalOutput").ap()

    P = Prog()
    with ExitStack() as es:
        def sb(name, shape, dt):
            return es.enter_context(nc.sbuf_tensor(name, shape, dt))

        def ps(name, shape, dt):
            return es.enter_context(nc.psum_tensor(name, shape, dt))

        DMASEMS = (["ldx%d%d" % (b, q) for b in range(2) for q in range(4)] + ["ldt%d" % q for q in range(4)] + ["ldm%d" % q for q in range(3)]
                   + ["ldw", "ldo", "ldo2", "ldc", "ldr0", "ldr1", "ldg", "st0", "st1"])
        for s in list(ENGS) + DMASEMS:
            P.sem[s] = es.enter_context(nc.semaphore(s))

        hT = sb("hT", [128, 8, UT], BF16)
        mixT = sb("mixT", [128, 8, OWN], BF16)
        tab = sb("tab", [128, UT], F32)
        fw2 = sb("fw2", [128, 2048], F32)
        q16t = sb("q16t", [128, 4096], BF16)
        wS = sb("wS", [128, 8, 512], BF16)
        kT = sb("kT", [128, UT], BF16)
        vT = sb("vT", [128, UT], BF16)
        qA = sb("qA", [128, OWN], BF16)
        qB = sb("qB", [128, OWN], BF16)
        gT = sb("gT", [128, OWN], BF16)
        NVB = 32
        Vp = sb("Vp", [128, NVB, 256], BF16)
        PT = sb("PT", [128, 4, 512], BF16)
        fw = sb("fw", [128, 4096], F32)
        qrot = sb("qrot", [128, 512], BF16)
        masks = sb("masks_sb", [128, 3, 512], BF16)
        ident = sb("ident", [128, 128], BF16)
        mfull = sb("mfull", [128, 2, 512], BF16)
        ones = sb("ones", [128, 128], BF16)
        consts = sb("consts_sb", [128, 16], F32)
        small = sb("small", [128, 8], F32)

        psI = ps("psI", [128, 1024], F32)
        psS = ps("psS", [128, 1024], F32)
        psO = ps("psO", [128, 1024], F32)
        psT = ps("psT", [128, 2048], BF16)
        block = es.enter_context(nc.Block())

        P.dma("sp", "ldc", consts[:], cs_d, writes=["consts"])
        identf = fw[:, 0:128]
        P.op("pool", lambda h: h.memset(identf[:], 0.0), writes=[("fw", 0)])
        P.op("pool", lambda h: h.affine_select(identf[:], identf[:], pattern=[[-1, 128]], compare_op=ALU.not_equal,
                                               fill=1.0, base=0, channel_multiplier=1),
             reads=[("fw", 0)], writes=[("fw", 0)])
        P.op("dve", lambda h: h.tensor_copy(ident[:], identf[:]), reads=[("fw", 0)], writes=["ident"])
        P.op("pool", lambda h: h.memset(ones[:], 1.0), writes=["ones"])
        P.op("pool", lambda h: h.memset(mfull[:], 0.0), writes=["mfull"])
        for (p0, a) in ((0, 0), (64, 0), (32, 1), (96, 1)):
            P.op("pool", lambda h, p0=p0, a=a: h.memset(mfull[p0:p0 + 32, a, :], 1.0), writes=["mfull"])
        Vp4 = Vp[:].rearrange("p n (a b) -> p n a b", b=64)
        P.op("pool", lambda h: h.memset(Vp4[:, :, 1, :], 1.0), writes=[("Vp", i) for i in range(NVB)])
        P.op("pool", lambda h: h.memset(Vp4[:, :, 3, :], 1.0), writes=[("Vp", i) for i in range(NVB)])

        def dump(name, ap, shape, dt, keys):
            if not debug:
                return
            t = nc.dram_tensor("dbg_" + name, list(shape), dt, kind="ExternalOutput").ap()
            dbg[name] = t
            P.dma("sp", "st0", t, ap, reads=keys)

        q16 = [q16t[:, 0:2048], q16t[:, 2048:4096]]

        gpre = consts[:, 0:8]
        maskA = consts[:, 14:15]
        maskB = consts[:, 15:16]

        def mm(out, lhsT, rhs, start, stop, r, w, signal=True, sgc=False):
            P.op("pe", lambda h: h.matmul(out, lhsT, rhs, start=start, stop=stop, skip_group_check=sgc), r, w, signal)

        def tr(out, in_, r, w, signal=True):
            P.op("pe", lambda h: h.transpose(out, in_, ident[:]), r, w, signal)

        def act(out, in_, func, r, w, **kw):
            P.op("act", lambda h: h.activation(out, in_, func, **kw), r, w)

        def tt(eng, out, in0, in1, op, r, w):
            P.op(eng, lambda h: h.tensor_tensor(out, in0, in1, op), r, w)

        def tsm(eng, out, in0, sc, r, w):
            P.op(eng, lambda h: h.tensor_scalar_mul(out, in0, sc), r, w)

        def stt(eng, out, in0, sc, in1, op0, op1, r, w):
            P.op(eng, lambda h: h.scalar_tensor_tensor(out, in0, sc, in1, op0, op1), r, w)

        def recip(out, in_, r, w):
            P.op("dve", lambda h: h.reciprocal(out, in_), r, w)

        def recip2(out, in_, scratch, r, w, ws):
            P.op("dve", lambda h: h.reciprocal_approx_fast(out=scratch, in_=in_), r, ws)
            P.op("dve", lambda h: h._custom_dve(bass.dve_ops.RECIPROCAL_APPROX_NR, out=out, in0=in_, in1=scratch, s0=2.0), list(r) + list(ws), w)

        def cpy(eng, out, in_, r, w):
            P.op(eng, lambda h: h.tensor_copy(out, in_), r, w)

        def mset(eng, ap, val, w):
            P.op(eng, lambda h: h.memset(ap, val), (), w)

        psI_state = [0]
        trb_state = [0]

        IBANKS = [(psI, 0, ("psI", 0)), (psI, 512, ("psI", 1)), (psS, 0, ("psS", 0)), (psS, 512, ("psS", 1))]

        def next_psI():
            b = psI_state[0]
            psI_state[0] = (b + 1) % 4
            return b

        def pI(bank):
            t_, o_, _ = IBANKS[bank]
            return t_[:, o_:o_ + 512]

        def pIk(bank):
            return IBANKS[bank][2]

        tmp_state = [0]

        def next_tmp():
            s = tmp_state[0]
            tmp_state[0] ^= 1
            base = 2048 + s * 1024
            return fw[:, base:base + 512], fw[:, base + 512:base + 1024], ("fw", 4 + 2 * s), ("fw", 5 + 2 * s)

        def wload(idx):
            P.dma("pool", "ldw", wS[:], w_d[:, idx * 512:(idx + 1) * 512].rearrange("(c p) f -> p c f", p=128), writes=["wS"])

        wS2 = kT[:, :].rearrange("p (c f) -> p c f", c=8)
        W2KEYS = [("kT", n) for n in range(8)]

        def inproj(colofs, n, bank, alt=False):
            wt = wS2 if alt else wS
            wk = W2KEYS if alt else ["wS"]
            for c in range(8):
                mm(pI(bank), wt[:, c, colofs:colofs + 128], hT[:, c, n * 512:(n + 1) * 512], c == 0, c == 7,
                   [("hT", n)] + wk, [pIk(bank)], signal=(c == 7))

        def rope(bank, n, out_ap, out_keys):
            m1, tp, km1, ktp = next_tmp()
            pv = pI(bank)
            cs = tab[0:64, n * 512:(n + 1) * 512]
            sn = tab[64:128, n * 512:(n + 1) * 512]
            rk = [pIk(bank), ("tab", n)]
            tt("dve", m1[0:64, :], pv[0:64, :], cs, ALU.mult, rk, [km1])
            tt("dve", m1[64:128, :], pv[64:128, :], cs, ALU.mult, rk, [km1])
            stt("dve", tp[0:64, :], pv[64:128, :], -1.0, sn, ALU.mult, ALU.mult, rk, [ktp])
            tt("dve", tp[64:128, :], pv[0:64, :], sn, ALU.mult, rk, [ktp])
            tt("pool", out_ap, m1, tp, ALU.add, [km1, ktp], out_keys)

        SBANKS = [(psS, 0, ("psS", 0)), (psS, 512, ("psS", 1)), (psI, 0, ("psI", 0)), (psI, 512, ("psI", 1))]
        mask_rr = [0]

        def emit_S(hf, sbank, d, qh, qname):
            qbs, mk, c, hh = hf
            pt_, po, pkey = SBANKS[sbank]
            sv = pt_[:, po:po + 512]
            (r0, b0), (r1, b1) = qbs
            if d == 16:
                jobs = [((r0, b0 - 1), 0, 1, (r0, b0)), ((r0, b0), 1, 1, (r0, b0)), ((r1, b1 - 1), 2, 1, (r1, b1)), ((r1, b1), 3, 1, (r1, b1))]
            else:
                jobs = [((r0, b0 - 1), 0, 1, (r0, b0)), ((r0, b0), 1, 2, (r0, b0)), ((r1, b1), 3, 1, (r1, b1))]
            for ji, ((kr, kb), sl0, nsl, (qr, qb)) in enumerate(jobs):
                k0 = OWN + kr + d * 128 * kb
                q0 = qr + d * 128 * qb
                nq = 128 * nsl
                last = (ji == len(jobs) - 1)
                if d == 16:
                    hd = 0 if qname == "qA" else 1
                    mv = q16[hd][:, qr * 128:(qr + 1) * 128]
                    mk_ = [("q16", hd, qr // 4)]
                else:
                    mv = qh[:, q0:q0 + (nq - 1) * d + 1:d]
                    mk_ = tk(qname, q0, q0 + (nq - 1) * d + 1)
                mm(pt_[:, po + sl0 * 128: po + (sl0 + nsl) * 128],
                   kT[:, k0:k0 + 127 * d + 1:d], mv, ji == 0, last,
                   tk("kT", k0, k0 + 127 * d + 1) + mk_, [pkey], signal=last, sgc=True)
            act(PT[:, sbank, :], sv, AF.Exp, [pkey], [("PT", sbank)], scale=0.125)
            eng = "pool" if (mask_rr[0] % 6 == 5 and mask_rr[0] >= 12) else "dve"
            mask_rr[0] += 1
            tt(eng, PT[:, sbank, :], PT[:, sbank, :], masks[:, mk, :], ALU.mult, [("PT", sbank), ("masks", mk)], [("PT", sbank)])

        def emit_PV(hf, sbank, obank, kbs, vcol, d):
            qbs, mk, c, hh = hf
            (r0, b0), (r1, b1) = qbs
            if d == 16:
                jobs = [((r0, b0 - 1), 0, 1, 0), ((r0, b0), 1, 1, 0), ((r1, b1 - 1), 2, 1, 1), ((r1, b1), 3, 1, 1)]
            else:
                jobs = [((r0, b0 - 1), 0, 1, 0), ((r0, b0), 1, 2, 0), ((r1, b1), 3, 1, 1)]
            for ji, (kkey, sl0, nsl, qi) in enumerate(jobs):
                oc = obank * 512 + (2 * hh + qi) * 128
                sl = kbs[kkey]
                first = (hh == 0 and ji == 0)
                lastj = (ji == len(jobs) - 1)
                mm(psO[:, oc:oc + 128 * nsl], Vp[:, sl, vcol:vcol + 128], PT[:, sbank, sl0 * 128:(sl0 + nsl) * 128], first, (hh == 1 and lastj),
                   [("Vp", sl), ("PT", sbank)], [("psO", obank)], signal=lastj, sgc=True)

        def emit_combine(c, obank, first_pat, d, acc, an):
            ov = psO[:, obank * 512:(obank + 1) * 512]
            if d == 1:
                av = acc[:, c * 512:(c + 1) * 512]
                akeys = [(an, c)]
            elif d == 4:
                av = acc[:, c:2048:4]
                akeys = [(an, j) for j in range(4)]
            else:
                av = acc.rearrange("p (m r) -> p r m", r=16)[:, 4 * c:4 * c + 4, :]
                ov = ov.rearrange("p (r m) -> p r m", r=4)
                akeys = [(an, j) for j in range(4)]
            if first_pat:
                act(av, ov, AF.Copy, [("psO", obank)], akeys)
            else:
                tt("dve", av, ov, av, ALU.add, [("psO", obank)] + akeys, akeys)

        SLOT_OFF = {1: 0, 4: 12, 16: 0}

        def pattern_slots(d):
            nb = 16 // d
            kbs = {}
            idx = 0
            for r in range(d):
                for b in range(-1, nb):
                    kbs[(r, b)] = (SLOT_OFF[d] + idx) % NVB
                    idx += 1
            return kbs

        def emit_transposes(d, items):
            for g0 in range(0, len(items), 4):
                grp = items[g0:g0 + 4]
                ng = len(grp)
                tb = trb_state[0]
                trb_state[0] ^= 1
                for j, ((r, b), sl) in enumerate(grp):
                    s0 = OWN + r + d * 128 * b
                    tr(psT[:, tb * 1024 + j * 128: tb * 1024 + (j + 1) * 128], vT[:, s0:s0 + 127 * d + 1:d],
                       tk("vT", s0, s0 + 127 * d + 1) + ["ident"], [("psT", tb)], signal=(j == ng - 1))
                for j, ((r, b), sl) in enumerate(grp):
                    dst = Vp[:, sl, :].rearrange("p (a b) -> p a b", b=64)[:, 0:4:2, :]
                    src = psT[:, tb * 1024 + j * 128: tb * 1024 + (j + 1) * 128].rearrange("p (a b) -> p a b", b=64)
                    pass
                sl0 = grp[0][1]
                contiguous = all(grp[j][1] == sl0 + j for j in range(ng))
                assert contiguous
                dst = Vp[:, sl0:sl0 + ng, :].rearrange("p n (a b) -> p n a b", b=64)[:, :, 0:4:2, :]
                src = psT[:, tb * 1024: tb * 1024 + ng * 128].rearrange("p (n a b) -> p n a b", a=2, b=64)
                if tb == 0:
                    act(dst, src, AF.Copy, [("psT", tb)], [("Vp", sl0 + i) for i in range(ng)])
                else:
                    cpy("dve", dst, src, [("psT", tb)], [("Vp", sl0 + i) for i in range(ng)])

        def attention(hp, u):
            accs = [(fw[:, 0:2048], "fw"), (fw2[:, 0:2048], "fw2")]
            mask_rr[0] = 0
            first_pat = True
            order = (1, 4, 16)
            slots = {d: pattern_slots(d) for d in order}
            used = {d: set(slots[d].values()) for d in order}
            early_done = {d: set() for d in order}
            for pi, d in enumerate(order):
                kbs = slots[d]
                items = sorted(kbs.items(), key=lambda kv: kv[1])
                emit_transposes(d, [it for it in items if it[1] not in early_done[d]])
                nxt = order[pi + 1] if pi + 1 < len(order) else None
                halves = []
                for head in range(2):
                    if d == 16:
                        for c in range(4):
                            for hh in range(2):
                                halves.append(([(4 * c + 2 * hh + i, 0) for i in range(2)], 2, c, hh, head))
                    elif d == 4:
                        for c in range(4):
                            for hh in range(2):
                                halves.append(([(c, 2 * hh + i) for i in range(2)], 1 if hh == 0 else 0, c, hh, head))
                    else:
                        for c in range(4):
                            for hh in range(2):
                                halves.append(([(0, 4 * c + 2 * hh + i) for i in range(2)], 1 if (c == 0 and hh == 0) else 0, c, hh, head))

                def S_of(hf, sbank):
                    head = hf[4]
                    emit_S(hf[:4], sbank, d, qA if head == 0 else qB, "qA" if head == 0 else "qB")

                nh = len(halves)
                for i in range(4):
                    S_of(halves[i], i)
                for i in range(nh):
                    hf = halves[i]
                    head = hf[4]
                    obank = (i // 2) % 2
                    emit_PV(hf[:4], i % 4, obank, kbs, 0 if head == 0 else 128, d)
                    if i + 4 < nh:
                        S_of(halves[i + 4], i % 4)
                    if i % 2 == 1:
                        emit_combine(hf[2], obank, first_pat, d, accs[head][0], accs[head][1])
                    if nxt is not None and i == nh - 9:
                        nitems = sorted(slots[nxt].items(), key=lambda kv: kv[1])
                        early = [it for it in nitems if it[1] not in used[d]]
                        emit_transposes(nxt, early)
                        early_done[nxt] = set(it[1] for it in early)
                first_pat = False
            for head in range(2):
                for j in range(4):
                    pending_fin.append((hp, head, j, accs))

        pending_fin = []

        def finalize_piece(hp, head, j, accs):
            acc, an = accs[head]
            hr = slice(0, 64) if head == 0 else slice(64, 128)
            half = j // 2
            rb = psO[half * 64:(half + 1) * 64, (j % 2) * 512:(j % 2 + 1) * 512]
            rk = ("psO", j % 2)
            act(rb, acc[64:128, j * 512:(j + 1) * 512], AF.Ln, [(an, j)], [rk])
            act(rb, rb, AF.Exp, [rk], [rk], scale=-1.0)
            tt("dve", acc[hr, j * 512:(j + 1) * 512], acc[0:64, j * 512:(j + 1) * 512], rb, ALU.mult, [(an, j), rk], [(an, j)])
            tt("pool", mixT[hr, hp, j * 512:(j + 1) * 512], acc[hr, j * 512:(j + 1) * 512], gT[hr, j * 512:(j + 1) * 512], ALU.mult,
               [(an, j), ("gT", j)], [("mix", hp, j, head)])

        def flush_fin(n=None):
            k = len(pending_fin) if n is None else min(n, len(pending_fin))
            for _ in range(k):
                finalize_piece(*pending_fin.pop(0))

        def conv_phase():
            ub = fw[:, 0:2560]
            chs = fw[:, 2560:3072]
            sg = fw[:, 3072:3584]
            y = fw[:, 3584:4096]
            for ct in range(2):
                cw = consts[:, 8 + 3 * ct: 11 + 3 * ct]
                alt = (ct == 1)
                for n in range(3, 8):
                    j0 = (n - 3) * 512
                    bank = next_psI()
                    inproj(0, n, bank, alt)
                    act(chs, pI(bank), AF.Copy, [pIk(bank)], [("fw", 5)])
                    bank = next_psI()
                    inproj(256, n, bank, alt)
                    tt("dve", ub[:, j0:j0 + 512], pI(bank), chs, ALU.mult, [pIk(bank), ("fw", 5)], [("fw", n - 3)])
                    flush_fin(2)
                for m in range(4):
                    n = m + 4
                    j0 = (m + 1) * 512
                    bank = next_psI()
                    inproj(384, n, bank, alt)
                    act(sg, pI(bank), AF.Silu, [pIk(bank)], [("fw", 6)])
                    ukeys = [("fw", m), ("fw", m + 1)]
                    act(y, ub[:, j0 - 2:j0 + 510], AF.Copy, ukeys + ["consts"], [("fw", 7)], scale=cw[:, 0:1])
                    stt("dve", y, ub[:, j0 - 1:j0 + 511], cw[:, 1:2], y, ALU.mult, ALU.add, ukeys + [("fw", 7), "consts"], [("fw", 7)])
                    stt("dve", y, ub[:, j0:j0 + 512], cw[:, 2:3], y, ALU.mult, ALU.add, ukeys + [("fw", 7), "consts"], [("fw", 7)])
                    bank = next_psI()
                    inproj(128, n, bank, alt)
                    tt("dve", y, pI(bank), y, ALU.mult, [pIk(bank), ("fw", 7)], [("fw", 7)])
                    tt("pool", mixT[:, 6 + ct, m * 512:(m + 1) * 512], y, sg, ALU.mult, [("fw", 6), ("fw", 7)],
                       [("mix", 6 + ct, m, 0), ("mix", 6 + ct, m, 1)])

        def unit(u):
            if u == 0:
                wload(0)
            for i in range(3):
                P.dma("pool", "ldm%d" % i, masks[:, i, :], mk_d[u, i], writes=[("masks", i)])

            PW = 256
            xsb = [tab[:, 0:2048].rearrange("p (c t) -> p c t", c=8), tab[:, 2048:4096].rearrange("p (c t) -> p c t", c=8),
                   fw[:, 0:2048].rearrange("p (c t) -> p c t", c=8), fw[:, 2048:4096].rearrange("p (c t) -> p c t", c=8)]
            xkn = [("tab", 0), ("tab", 4), ("fw", 0), ("fw", 4)]
            vflat = Vp[:].rearrange("p n f -> p (n f)")
            sqb = [vflat[:, i * 2048:(i + 1) * 2048].rearrange("p (c t) -> p c t", c=8) for i in range(4)]
            sqk = [[("Vp", i * 8 + j) for j in range(8)] for i in range(4)]
            xsrc = xT_d[u].rearrange("(c p) t -> p c t", p=128)
            vpkeys = [("Vp", i) for i in range(32)]
            NPC = UT // PW

            def p1_bufs(n):
                i = n % 4
                kn, k0 = xkn[i]
                xk = [(kn, k0 + j) for j in range(4)]
                j2 = n % 2
                return (xsb[i], xk, sqb[i], sqk[i], fw2[:, j2 * 1024:j2 * 1024 + PW], fw2[:, j2 * 1024 + 512:j2 * 1024 + 512 + PW],
                        [("fw2", 2 * j2)], [("fw2", 2 * j2 + 1)])

            def p1_A1(n):
                xs, xk, sq, sk, rsb, rsc, krs, krc = p1_bufs(n)
                for q in range(2):
                    P.dma("sp", "ldx%d%d" % (n % 4 // 2, 2 * (n % 2) + q), xs[:, 4 * q:4 * q + 4, :], xsrc[:, 4 * q:4 * q + 4, n * PW:(n + 1) * PW],
                          writes=xk[2 * q:2 * q + 2])
                act(sq, xs, AF.Square, xk, sk)
                for c in range(5, 8):
                    act(xs[:, c, :], xs[:, c, :], AF.Copy, [xk[c // 2], "consts"], [xk[c // 2]], scale=gpre[:, c:c + 1])
                bank = next_psI()
                for c in range(8):
                    mm(pI(bank)[:, 0:PW], ones[:], sq[:, c, :], c == 0, c == 7, sk + ["ones"], [pIk(bank)], signal=(c == 7))
                act(rsc, pI(bank)[:, 0:PW], AF.Sqrt, [pIk(bank)], krc, bias=EPS, scale=1.0 / D)

            def p1_A2(n):
                xs, xk, sq, sk, rsb, rsc, krs, krc = p1_bufs(n)
                recip(rsb, rsc, krc, krs)

            def p1_B(n):
                xs, xk, sq, sk, rsb, rsc, krs, krc = p1_bufs(n)
                hk = [("hT", (n * PW) // 512)]
                for c in range(8):
                    dst = hT[:, c, n * PW:(n + 1) * PW]
                    if c < 5:
                        stt("dve", dst, xs[:, c, :], gpre[:, c:c + 1], rsb, ALU.mult, ALU.mult, [xk[c // 2], "consts"] + krs, hk)
                    else:
                        tt("pool", dst, xs[:, c, :], rsb, ALU.mult, [xk[c // 2]] + krs, hk)

            p1_A1(0)
            p1_A2(0)
            for n in range(1, NPC):
                p1_A1(n)
                p1_B(n - 1)
                p1_A2(n)
            p1_B(NPC - 1)
            for q in range(4):
                P.dma("sp", "ldt%d" % q, tab[:, q * 1024:(q + 1) * 1024], tab_d[u, :, q * 1024:(q + 1) * 1024],
                      writes=[("tab", 2 * q), ("tab", 2 * q + 1)])
            if u == 1:
                dump("hT", hT[:], [128, 8, UT], BF16, [("hT", n) for n in range(8)])
            mset("pool", Vp4[:, :, 1, :], 1.0, vpkeys)
            mset("pool", Vp4[:, :, 3, :], 1.0, vpkeys)

            for hp in range(6):
                for n in range(8):
                    bank = next_psI()
                    inproj(0, n, bank)
                    rope(bank, n, kT[:, n * 512:(n + 1) * 512], [("kT", n)])
                    bank = next_psI()
                    inproj(128, n, bank)
                    act(vT[:, n * 512:(n + 1) * 512], pI(bank), AF.Copy, [pIk(bank)], [("vT", n)])
                    if n < 4:
                        flush_fin(2)
                    if n >= 4:
                        m = n - 4
                        bank = next_psI()
                        inproj(256, n, bank)
                        rope(bank, n, qrot[:], ["qrot"])
                        tt("dve", qA[:, m * 512:(m + 1) * 512], qrot[:], mfull[:, 0, :], ALU.mult, ["qrot", "mfull"], [("qA", m)])
                        tt("pool", qB[:, m * 512:(m + 1) * 512], qrot[:], mfull[:, 1, :], ALU.mult, ["qrot", "mfull"], [("qB", m)])
                        for hd, (qsrc, qn) in enumerate(((qA, "qA"), (qB, "qB"))):
                            cpy("pool", q16[hd].rearrange("p (r m) -> p r m", m=128)[:, :, 32 * m:32 * m + 32],
                                qsrc[:, m * 512:(m + 1) * 512].rearrange("p (m r) -> p r m", r=16),
                                [(qn, m)], [("q16", hd, j) for j in range(4)])
                        bank = next_psI()
                        inproj(384, n, bank)
                        act(gT[:, m * 512:(m + 1) * 512], pI(bank), AF.Silu, [pIk(bank)], [("gT", m)])
                if u == 1 and hp == 0:
                    dump("kT", kT[:], [128, UT], BF16, [("kT", n) for n in range(8)])
                    dump("vT", vT[:], [128, UT], BF16, [("vT", n) for n in range(8)])
                    dump("qA", qA[:], [128, OWN], BF16, [("qA", n) for n in range(4)])
                    dump("qB", qB[:], [128, OWN], BF16, [("qB", n) for n in range(4)])
                    dump("gT", gT[:], [128, OWN], BF16, [("gT", n) for n in range(4)])
                wload(hp + 1)
                if hp == 5:
                    P.dma("pool", "ldo", hT[:, :, 0:1024], wo_d.rearrange("(c p) f -> p c f", p=128), writes=[("hT", 0), ("hT", 1)])
                attention(hp, u)

            P.dma("pool", "ldo2", wS2, w_d[:, 7 * 512:8 * 512].rearrange("(c p) f -> p c f", p=128), writes=W2KEYS)
            flush_fin(4)
            conv_phase()
            flush_fin()

            if u == 1:
                dump("mixT", mixT[:], [128, 8, OWN], BF16, [("mix", c, j, hd) for c in range(8) for j in range(4) for hd in range(2)])
            wout = hT[:, :, 0:1024]
            if u == 0:
                wload(0)
            gpb = tab[:, 0:1024]
            P.dma("sp", "ldg", gpb, gp_d, writes=[("tab", 0), ("tab", 1)])
            junk = fw2[:, 0:1024]
            for t in range(16):
                i = t % 2
                pb, pbn = ((psS, "psS"), (psO, "psO"), (psI, "psI"))[t % 3]
                xres = fw[:, i * 1024:(i + 1) * 1024]
                ob = fw[:, 2048 + i * 1024:2048 + (i + 1) * 1024]
                kx = [("fw", 2 * i), ("fw", 2 * i + 1)]
                ko = [("fw", 4 + 2 * i), ("fw", 5 + 2 * i)]
                ss = small[:, i:i + 1]
                kss = ("ss", i)
                if t == 0:
                    P.dma("sp", "ldr0", xres, xo_d[u, 0:128, :], writes=kx)
                if t + 1 < 16:
                    i1 = (t + 1) % 2
                    P.dma("sp", "ldr%d" % i1, fw[:, i1 * 1024:(i1 + 1) * 1024], xo_d[u, (t + 1) * 128:(t + 2) * 128, :],
                          writes=[("fw", 2 * i1), ("fw", 2 * i1 + 1)])
                for half in range(2):
                    for c in range(8):
                        mm(pb[:, half * 512:(half + 1) * 512], mixT[:, c, t * 128:(t + 1) * 128], wout[:, c, half * 512:(half + 1) * 512],
                           c == 0, c == 7, [("mix", c, t // 4, 0), ("mix", c, t // 4, 1), ("hT", 0), ("hT", 1)], [(pbn, half)], signal=(c == 7))
                act(junk, pb[:, :], AF.Square, [(pbn, 0), (pbn, 1)], [("fw2", 0), ("fw2", 1)])
                P.op("dve", lambda h, ss=ss: h.reduce_sum(ss, junk, mybir.AxisListType.X), [("fw2", 0), ("fw2", 1)], [kss])
                act(ss, ss, AF.Sqrt, [kss], [kss], bias=EPS, scale=1.0 / D)
                recip(ss, ss, [kss], [kss])
                stt("dve", ob, pb[:, :], ss, gpb, ALU.mult, ALU.mult, [(pbn, 0), (pbn, 1), kss, ("tab", 0), ("tab", 1)], ko)
                tt("pool", ob, ob, xres, ALU.add, kx + ko, ko)
                P.dma("sp", "st%d" % i, out_d[u, t * 128:(t + 1) * 128, :], ob, reads=ko)

        for u in range(2):
            unit(u)
        for e in ENGS:
            P.wait_all_dma(e, ["st0", "st1"])
        P.run(block)
    return nc


def _col_perm():
    perm = []
    for hp in range(6):
        hA, hB = 2 * hp, 2 * hp + 1
        rot = ([hA * 64 + i for i in range(32)] + [hB * 64 + i for i in range(32)]
               + [hA * 64 + 32 + i for i in range(32)] + [hB * 64 + 32 + i for i in range(32)])
        perm += [768 + c for c in rot]
        perm += [1536 + hp * 128 + i for i in range(128)]
        perm += rot
        perm += [2304 + hp * 128 + i for i in range(128)]
    for ct in range(2):
        for k4 in range(4):
            perm += [3072 + k4 * 256 + ct * 128 + i for i in range(128)]
    return np.asarray(perm)


def _tables(sb):
    posi = np.arange(UT, dtype=np.int64) + sb * OWN - OWN
    try:
        import jax
        import jax.numpy as jnp
        with jax.default_device(jax.devices("cpu")[0]):
            inv_freq = 10000.0 ** (-jnp.arange(32, dtype=jnp.float32) * 2.0 / 64)
            ang = jnp.asarray(posi.astype(np.int32)).astype(jnp.float32)[:, None] * inv_freq[None, :]
            c = np.asarray(jnp.cos(ang), dtype=np.float32).T
            s = np.asarray(jnp.sin(ang), dtype=np.float32).T
    except Exception:
        inv_freq = (np.float32(10000.0) ** (-np.arange(32, dtype=np.float32) * np.float32(2.0) / np.float32(64))).astype(np.float32)
        ang = (posi.astype(np.float32)[None, :] * inv_freq[:, None]).astype(np.float32)
        c = np.cos(ang).astype(np.float32)
        s = np.sin(ang).astype(np.float32)
    return np.ascontiguousarray(np.concatenate([c, c, s, s], axis=0))


def _masks(sb):
    p = np.arange(128)[:, None]
    i = np.arange(128)[None, :]
    M1 = np.where(p >= i, 1.0, 0.0).astype(np.float32)
    M2 = np.where(p <= i, 1.0, 0.0).astype(np.float32)
    MH = M1 if sb > 0 else np.zeros((128, 128), np.float32)
    MA = np.concatenate([M1, M2, M1, M2], axis=1)
    MB = np.concatenate([MH, M2, M1, M2], axis=1)
    MC = np.concatenate([MH, M2, MH, M2], axis=1)
    return np.stack([MA, MB, MC], axis=0)


_NC_CACHE = {}


def kernel(x, norm_pre_g, w_in, conv_w, w_out, norm_post_g):
    x = np.ascontiguousarray(np.asarray(x, dtype=np.float32))
    w_in = np.asarray(w_in, dtype=np.float32)
    wr = np.ascontiguousarray(w_in[:, _col_perm()])
    wout = np.ascontiguousarray(np.asarray(w_out, dtype=np.float32))
    gpost = np.ascontiguousarray(np.tile(np.asarray(norm_post_g, np.float32)[None, :], (128, 1)))
    consts = np.zeros((128, 16), np.float32)
    consts[:, 0:8] = np.asarray(norm_pre_g, np.float32).reshape(8, 128).T
    cw = np.asarray(conv_w, np.float32)
    for ct in range(2):
        consts[:, 8 + 3 * ct: 11 + 3 * ct] = cw[:, ct * 128:(ct + 1) * 128].T
    pp = np.arange(128)
    consts[:, 14] = ((pp // 32) % 2 == 0).astype(np.float32)
    consts[:, 15] = ((pp // 32) % 2 == 1).astype(np.float32)

    in_maps = []
    for c in range(NCORES):
        xT = np.zeros((2, D, UT), np.float32)
        xo = np.zeros((2, OWN, D), np.float32)
        tabs = np.zeros((2, 128, UT), np.float32)
        mk = np.zeros((2, 3, 128, 512), np.float32)
        for uu in range(2):
            g = 2 * c + uu
            b, sbk = g // 4, g % 4
            own = x[b, sbk * OWN:(sbk + 1) * OWN]
            xo[uu] = own
            xT[uu, :, OWN:] = own.T
            if sbk > 0:
                xT[uu, :, :OWN] = x[b, (sbk - 1) * OWN: sbk * OWN].T
            tabs[uu] = _tables(sbk)
            mk[uu] = _masks(sbk)
        in_maps.append({"xT": xT, "xown": xo, "tabs": tabs, "wr": wr, "wout": wout, "gpost": gpost,
                        "consts": consts, "masks": mk})
    if "nc" not in _NC_CACHE:
        _NC_CACHE["nc"] = build_nc()
    nc = _NC_CACHE["nc"]
    res = run_bass_kernel_spmd(nc, in_maps, core_ids=list(range(NCORES)))
    out = np.zeros((BATCH, SEQ, D), np.float32)
    for c in range(NCORES):
        o = np.asarray(res.results[c]["out"], dtype=np.float32)
        for uu in range(2):
            g = 2 * c + uu
            b, sbk = g // 4, g % 4
            out[b, sbk * OWN:(sbk + 1) * OWN] = o[uu]
    return out
```

```python
from contextlib import ExitStack

import numpy as np
import ml_dtypes

import concourse.bass as bass
import concourse.mybir as mybir
from concourse.bass_utils import run_bass_kernel_spmd

F32 = mybir.dt.float32
BF16 = mybir.dt.bfloat16
ALU = mybir.AluOpType
AF = mybir.ActivationFunctionType

ENGS = ("pe", "act", "dve", "pool", "sp")
NCORES = 8
D = 1024
SEQ = 8192
BATCH = 4
OWN = 2048
UT = 4096
NEG = -30000.0
EPS = 1e-6


class Prog:
    def __init__(self):
        self.ops = {e: [] for e in ENGS}
        self.cnt = {e: 0 for e in ENGS}
        self.seen = {e: {} for e in ENGS}
        self.bufs = {}
        self.sem = {}
        self.dma_cnt = {}

    def _wait(self, eng, ev):
        kind, key, val = ev
        if kind == "eng" and key == eng and eng != "pool":
            return
        k = (kind, key)
        if self.seen[eng].get(k, 0) >= val:
            return
        self.seen[eng][k] = val
        semh = self.sem[key]
        self.ops[eng].append(lambda h, semh=semh, val=val: h.wait_ge(semh, val))

    def _deps(self, eng, reads, writes):
        for b in reads:
            st = self.bufs.get(b)
            if st and st["w"]:
                self._wait(eng, st["w"])
        for b in writes:
            st = self.bufs.get(b)
            if st:
                if st["w"]:
                    self._wait(eng, st["w"])
                for r in st["r"]:
                    self._wait(eng, r)

    def _upd(self, ev, reads, writes):
        for b in reads:
            st = self.bufs.setdefault(b, {"w": None, "r": []})
            st["r"].append(ev)
        for b in writes:
            self.bufs[b] = {"w": ev, "r": []}

    def op(self, eng, fn, reads=(), writes=(), signal=True):
        self._deps(eng, reads, writes)
        if signal:
            self.cnt[eng] += 1
            ev = ("eng", eng, self.cnt[eng])
            semh = self.sem[eng]
            self.ops[eng].append(lambda h, fn=fn, semh=semh: fn(h).then_inc(semh, 1))
        else:
            ev = ("eng", eng, self.cnt[eng] + 1)
            self.ops[eng].append(lambda h, fn=fn: fn(h))
        self._upd(ev, reads, writes)

    def dma(self, q, semname, out, in_, reads=(), writes=()):
        self._deps(q, reads, writes)
        self.dma_cnt[semname] = self.dma_cnt.get(semname, 0) + 16
        val = self.dma_cnt[semname]
        ev = ("dma", semname, val)
        semh = self.sem[semname]
        self.ops[q].append(lambda h, out=out, in_=in_, semh=semh: h.dma_start(out, in_).then_inc(semh, 16))
        self._upd(ev, reads, writes)

    def wait_all_dma(self, eng, semnames):
        for s in semnames:
            if s in self.dma_cnt:
                self._wait(eng, ("dma", s, self.dma_cnt[s]))

    def run(self, block):
        m = {"pe": block.tensor, "act": block.scalar, "dve": block.vector, "pool": block.gpsimd, "sp": block.sync}
        for e in ENGS:
            lst = self.ops[e]

            def body(h, lst=lst):
                for f in lst:
                    f(h)

            m[e](body)


def tk(name, lo, hi, g=512):
    return [(name, j) for j in range(lo // g, (hi - 1) // g + 1)]


def build_nc(debug=False):
    nc = bass.Bass("TRN2", target_bir_lowering=False)
    dbg = {}
    xT_d = nc.dram_tensor("xT", [2, D, UT], F32, kind="ExternalInput").ap()
    xo_d = nc.dram_tensor("xown", [2, OWN, D], F32, kind="ExternalInput").ap()
    tab_d = nc.dram_tensor("tabs", [2, 128, UT], F32, kind="ExternalInput").ap()
    w_d = nc.dram_tensor("wr", [D, 4096], F32, kind="ExternalInput").ap()
    wo_d = nc.dram_tensor("wout", [D, D], F32, kind="ExternalInput").ap()
    gp_d = nc.dram_tensor("gpost", [128, D], F32, kind="ExternalInput").ap()
    cs_d = nc.dram_tensor("consts", [128, 16], F32, kind="ExternalInput").ap()
    mk_d = nc.dram_tensor("masks", [2, 3, 128, 512], F32, kind="ExternalInput").ap()
    out_d = nc.dram_tensor("out", [2, OWN, D], F32, kind="ExternalOutput").ap()

    P = Prog()
    with ExitStack() as es:
        def sb(name, shape, dt):
            return es.enter_context(nc.sbuf_tensor(name, shape, dt))

        def ps(name, shape, dt):
            return es.enter_context(nc.psum_tensor(name, shape, dt))

        DMASEMS = (["ldx%d%d" % (b, q) for b in range(2) for q in range(4)] + ["ldt%d" % q for q in range(4)] + ["ldm%d" % q for q in range(3)]
                   + ["ldw", "ldo", "ldo2", "ldc", "ldr0", "ldr1", "ldg", "st0", "st1"])
        for s in list(ENGS) + DMASEMS:
            P.sem[s] = es.enter_context(nc.semaphore(s))

        hT = sb("hT", [128, 8, UT], BF16)
        mixT = sb("mixT", [128, 8, OWN], BF16)
        tab = sb("tab", [128, UT], F32)
        fw2 = sb("fw2", [128, 2048], F32)
        q16t = sb("q16t", [128, 4096], BF16)
        wS = sb("wS", [128, 8, 512], BF16)
        kT = sb("kT", [128, UT], BF16)
        vT = sb("vT", [128, UT], BF16)
        qA = sb("qA", [128, OWN], BF16)
        qB = sb("qB", [128, OWN], BF16)
        gT = sb("gT", [128, OWN], BF16)
        NVB = 32
        Vp = sb("Vp", [128, NVB, 256], BF16)
        PT = sb("PT", [128, 4, 512], BF16)
        fw = sb("fw", [128, 4096], F32)
        qrot = sb("qrot", [128, 512], BF16)
        masks = sb("masks_sb", [128, 3, 512], BF16)
        ident = sb("ident", [128, 128], BF16)
        mfull = sb("mfull", [128, 2, 512], BF16)
        ones = sb("ones", [128, 128], BF16)
        consts = sb("consts_sb", [128, 16], F32)
        small = sb("small", [128, 8], F32)

        psI = ps("psI", [128, 1024], F32)
        psS = ps("psS", [128, 1024], F32)
        psO = ps("psO", [128, 1024], F32)
        psT = ps("psT", [128, 2048], BF16)
        block = es.enter_context(nc.Block())

        P.dma("sp", "ldc", consts[:], cs_d, writes=["consts"])
        identf = fw[:, 0:128]
        P.op("pool", lambda h: h.memset(identf[:], 0.0), writes=[("fw", 0)])
        P.op("pool", lambda h: h.affine_select(identf[:], identf[:], pattern=[[-1, 128]], compare_op=ALU.not_equal,
                                               fill=1.0, base=0, channel_multiplier=1),
             reads=[("fw", 0)], writes=[("fw", 0)])
        P.op("dve", lambda h: h.tensor_copy(ident[:], identf[:]), reads=[("fw", 0)], writes=["ident"])
        P.op("pool", lambda h: h.memset(ones[:], 1.0), writes=["ones"])
        P.op("pool", lambda h: h.memset(mfull[:], 0.0), writes=["mfull"])
        for (p0, a) in ((0, 0), (64, 0), (32, 1), (96, 1)):
            P.op("pool", lambda h, p0=p0, a=a: h.memset(mfull[p0:p0 + 32, a, :], 1.0), writes=["mfull"])
        Vp4 = Vp[:].rearrange("p n (a b) -> p n a b", b=64)
        P.op("pool", lambda h: h.memset(Vp4[:, :, 1, :], 1.0), writes=[("Vp", i) for i in range(NVB)])
        P.op("pool", lambda h: h.memset(Vp4[:, :, 3, :], 1.0), writes=[("Vp", i) for i in range(NVB)])

        def dump(name, ap, shape, dt, keys):
            if not debug:
                return
            t = nc.dram_tensor("dbg_" + name, list(shape), dt, kind="ExternalOutput").ap()
            dbg[name] = t
            P.dma("sp", "st0", t, ap, reads=keys)

        q16 = [q16t[:, 0:2048], q16t[:, 2048:4096]]

        gpre = consts[:, 0:8]
        maskA = consts[:, 14:15]
        maskB = consts[:, 15:16]

        def mm(out, lhsT, rhs, start, stop, r, w, signal=True, sgc=False):
            P.op("pe", lambda h: h.matmul(out, lhsT, rhs, start=start, stop=stop, skip_group_check=sgc), r, w, signal)

        def tr(out, in_, r, w, signal=True):
            P.op("pe", lambda h: h.transpose(out, in_, ident[:]), r, w, signal)

        def act(out, in_, func, r, w, **kw):
            P.op("act", lambda h: h.activation(out, in_, func, **kw), r, w)

        def tt(eng, out, in0, in1, op, r, w):
            P.op(eng, lambda h: h.tensor_tensor(out, in0, in1, op), r, w)

        def tsm(eng, out, in0, sc, r, w):
            P.op(eng, lambda h: h.tensor_scalar_mul(out, in0, sc), r, w)

        def stt(eng, out, in0, sc, in1, op0, op1, r, w):
            P.op(eng, lambda h: h.scalar_tensor_tensor(out, in0, sc, in1, op0, op1), r, w)

        def recip(out, in_, r, w):
            P.op("dve", lambda h: h.reciprocal(out, in_), r, w)

        def recip2(out, in_, scratch, r, w, ws):
            P.op("dve", lambda h: h.reciprocal_approx_fast(out=scratch, in_=in_), r, ws)
            P.op("dve", lambda h: h._custom_dve(bass.dve_ops.RECIPROCAL_APPROX_NR, out=out, in0=in_, in1=scratch, s0=2.0), list(r) + list(ws), w)

        def cpy(eng, out, in_, r, w):
            P.op(eng, lambda h: h.tensor_copy(out, in_), r, w)

        def mset(eng, ap, val, w):
            P.op(eng, lambda h: h.memset(ap, val), (), w)

        psI_state = [0]
        trb_state = [0]

        IBANKS = [(psI, 0, ("psI", 0)), (psI, 512, ("psI", 1)), (psS, 0, ("psS", 0)), (psS, 512, ("psS", 1))]

        def next_psI():
            b = psI_state[0]
            psI_state[0] = (b + 1) % 4
            return b

        def pI(bank):
            t_, o_, _ = IBANKS[bank]
            return t_[:, o_:o_ + 512]

        def pIk(bank):
            return IBANKS[bank][2]

        tmp_state = [0]

        def next_tmp():
            s = tmp_state[0]
            tmp_state[0] ^= 1
            base = 2048 + s * 1024
            return fw[:, base:base + 512], fw[:, base + 512:base + 1024], ("fw", 4 + 2 * s), ("fw", 5 + 2 * s)

        def wload(idx):
            P.dma("pool", "ldw", wS[:], w_d[:, idx * 512:(idx + 1) * 512].rearrange("(c p) f -> p c f", p=128), writes=["wS"])

        wS2 = kT[:, :].rearrange("p (c f) -> p c f", c=8)
        W2KEYS = [("kT", n) for n in range(8)]

        def inproj(colofs, n, bank, alt=False):
            wt = wS2 if alt else wS
            wk = W2KEYS if alt else ["wS"]
            for c in range(8):
                mm(pI(bank), wt[:, c, colofs:colofs + 128], hT[:, c, n * 512:(n + 1) * 512], c == 0, c == 7,
                   [("hT", n)] + wk, [pIk(bank)], signal=(c == 7))

        def rope(bank, n, out_ap, out_keys):
            m1, tp, km1, ktp = next_tmp()
            pv = pI(bank)
            cs = tab[0:64, n * 512:(n + 1) * 512]
            sn = tab[64:128, n * 512:(n + 1) * 512]
            rk = [pIk(bank), ("tab", n)]
            tt("dve", m1[0:64, :], pv[0:64, :], cs, ALU.mult, rk, [km1])
            tt("dve", m1[64:128, :], pv[64:128, :], cs, ALU.mult, rk, [km1])
            stt("dve", tp[0:64, :], pv[64:128, :], -1.0, sn, ALU.mult, ALU.mult, rk, [ktp])
            tt("dve", tp[64:128, :], pv[0:64, :], sn, ALU.mult, rk, [ktp])
            tt("pool", out_ap, m1, tp, ALU.add, [km1, ktp], out_keys)

        SBANKS = [(psS, 0, ("psS", 0)), (psS, 512, ("psS", 1)), (psI, 0, ("psI", 0)), (psI, 512, ("psI", 1))]
        mask_rr = [0]

        def emit_S(hf, sbank, d, qh, qname):
            qbs, mk, c, hh = hf
            pt_, po, pkey = SBANKS[sbank]
            sv = pt_[:, po:po + 512]
            (r0, b0), (r1, b1) = qbs
            if d == 16:
                jobs = [((r0, b0 - 1), 0, 1, (r0, b0)), ((r0, b0), 1, 1, (r0, b0)), ((r1, b1 - 1), 2, 1, (r1, b1)), ((r1, b1), 3, 1, (r1, b1))]
            else:
                jobs = [((r0, b0 - 1), 0, 1, (r0, b0)), ((r0, b0), 1, 2, (r0, b0)), ((r1, b1), 3, 1, (r1, b1))]
            for ji, ((kr, kb), sl0, nsl, (qr, qb)) in enumerate(jobs):
                k0 = OWN + kr + d * 128 * kb
                q0 = qr + d * 128 * qb
                nq = 128 * nsl
                last = (ji == len(jobs) - 1)
                if d == 16:
                    hd = 0 if qname == "qA" else 1
                    mv = q16[hd][:, qr * 128:(qr + 1) * 128]
                    mk_ = [("q16", hd, qr // 4)]
                else:
                    mv = qh[:, q0:q0 + (nq - 1) * d + 1:d]
                    mk_ = tk(qname, q0, q0 + (nq - 1) * d + 1)
                mm(pt_[:, po + sl0 * 128: po + (sl0 + nsl) * 128],
                   kT[:, k0:k0 + 127 * d + 1:d], mv, ji == 0, last,
                   tk("kT", k0, k0 + 127 * d + 1) + mk_, [pkey], signal=last, sgc=True)
            act(PT[:, sbank, :], sv, AF.Exp, [pkey], [("PT", sbank)], scale=0.125)
            eng = "pool" if (mask_rr[0] % 4 == 3 and mask_rr[0] >= 12) else "dve"
            mask_rr[0] += 1
            tt(eng, PT[:, sbank, :], PT[:, sbank, :], masks[:, mk, :], ALU.mult, [("PT", sbank), ("masks", mk)], [("PT", sbank)])

        def emit_PV(hf, sbank, obank, kbs, vcol, d):
            qbs, mk, c, hh = hf
            (r0, b0), (r1, b1) = qbs
            if d == 16:
                jobs = [((r0, b0 - 1), 0, 1, 0), ((r0, b0), 1, 1, 0), ((r1, b1 - 1), 2, 1, 1), ((r1, b1), 3, 1, 1)]
            else:
                jobs = [((r0, b0 - 1), 0, 1, 0), ((r0, b0), 1, 2, 0), ((r1, b1), 3, 1, 1)]
            for ji, (kkey, sl0, nsl, qi) in enumerate(jobs):
                oc = obank * 512 + (2 * hh + qi) * 128
                sl = kbs[kkey]
                first = (hh == 0 and ji == 0)
                lastj = (ji == len(jobs) - 1)
                mm(psO[:, oc:oc + 128 * nsl], Vp[:, sl, vcol:vcol + 128], PT[:, sbank, sl0 * 128:(sl0 + nsl) * 128], first, (hh == 1 and lastj),
                   [("Vp", sl), ("PT", sbank)], [("psO", obank)], signal=lastj, sgc=True)

        def emit_combine(c, obank, first_pat, d, acc, an):
            ov = psO[:, obank * 512:(obank + 1) * 512]
            if d == 1:
                av = acc[:, c * 512:(c + 1) * 512]
                akeys = [(an, c)]
            elif d == 4:
                av = acc[:, c:2048:4]
                akeys = [(an, j) for j in range(4)]
            else:
                av = acc.rearrange("p (m r) -> p r m", r=16)[:, 4 * c:4 * c + 4, :]
                ov = ov.rearrange("p (r m) -> p r m", r=4)
                akeys = [(an, j) for j in range(4)]
            if first_pat:
                act(av, ov, AF.Copy, [("psO", obank)], akeys)
            else:
                tt("dve", av, ov, av, ALU.add, [("psO", obank)] + akeys, akeys)

        SLOT_OFF = {1: 0, 4: 12, 16: 0}

        def pattern_slots(d):
            nb = 16 // d
            kbs = {}
            idx = 0
            for r in range(d):
                for b in range(-1, nb):
                    kbs[(r, b)] = (SLOT_OFF[d] + idx) % NVB
                    idx += 1
            return kbs

        def emit_transposes(d, items):
            for g0 in range(0, len(items), 4):
                grp = items[g0:g0 + 4]
                ng = len(grp)
                tb = trb_state[0]
                trb_state[0] ^= 1
                for j, ((r, b), sl) in enumerate(grp):
                    s0 = OWN + r + d * 128 * b
                    tr(psT[:, tb * 1024 + j * 128: tb * 1024 + (j + 1) * 128], vT[:, s0:s0 + 127 * d + 1:d],
                       tk("vT", s0, s0 + 127 * d + 1) + ["ident"], [("psT", tb)], signal=(j == ng - 1))
                for j, ((r, b), sl) in enumerate(grp):
                    dst = Vp[:, sl, :].rearrange("p (a b) -> p a b", b=64)[:, 0:4:2, :]
                    src = psT[:, tb * 1024 + j * 128: tb * 1024 + (j + 1) * 128].rearrange("p (a b) -> p a b", b=64)
                    pass
                sl0 = grp[0][1]
                contiguous = all(grp[j][1] == sl0 + j for j in range(ng))
                assert contiguous
                dst = Vp[:, sl0:sl0 + ng, :].rearrange("p n (a b) -> p n a b", b=64)[:, :, 0:4:2, :]
                src = psT[:, tb * 1024: tb * 1024 + ng * 128].rearrange("p (n a b) -> p n a b", a=2, b=64)
                if tb == 0:
                    act(dst, src, AF.Copy, [("psT", tb)], [("Vp", sl0 + i) for i in range(ng)])
                else:
                    cpy("dve", dst, src, [("psT", tb)], [("Vp", sl0 + i) for i in range(ng)])

        def attention(hp, u):
            accs = [(fw[:, 0:2048], "fw"), (fw2[:, 0:2048], "fw2")]
            mask_rr[0] = 0
            first_pat = True
            order = (1, 4, 16)
            slots = {d: pattern_slots(d) for d in order}
            used = {d: set(slots[d].values()) for d in order}
            early_done = {d: set() for d in order}
            for pi, d in enumerate(order):
                kbs = slots[d]
                items = sorted(kbs.items(), key=lambda kv: kv[1])
                emit_transposes(d, [it for it in items if it[1] not in early_done[d]])
                nxt = order[pi + 1] if pi + 1 < len(order) else None
                halves = []
                for head in range(2):
                    if d == 16:
                        for c in range(4):
                            for hh in range(2):
                                halves.append(([(4 * c + 2 * hh + i, 0) for i in range(2)], 2, c, hh, head))
                    elif d == 4:
                        for c in range(4):
                            for hh in range(2):
                                halves.append(([(c, 2 * hh + i) for i in range(2)], 1 if hh == 0 else 0, c, hh, head))
                    else:
                        for c in range(4):
                            for hh in range(2):
                                halves.append(([(0, 4 * c + 2 * hh + i) for i in range(2)], 1 if (c == 0 and hh == 0) else 0, c, hh, head))

                def S_of(hf, sbank):
                    head = hf[4]
                    emit_S(hf[:4], sbank, d, qA if head == 0 else qB, "qA" if head == 0 else "qB")

                nh = len(halves)
                for i in range(4):
                    S_of(halves[i], i)
                for i in range(nh):
                    hf = halves[i]
                    head = hf[4]
                    obank = (i // 2) % 2
                    emit_PV(hf[:4], i % 4, obank, kbs, 0 if head == 0 else 128, d)
                    if i + 4 < nh:
                        S_of(halves[i + 4], i % 4)
                    if i % 2 == 1:
                        emit_combine(hf[2], obank, first_pat, d, accs[head][0], accs[head][1])
                    if nxt is not None and i == nh - 9:
                        nitems = sorted(slots[nxt].items(), key=lambda kv: kv[1])
                        early = [it for it in nitems if it[1] not in used[d]]
                        emit_transposes(nxt, early)
                        early_done[nxt] = set(it[1] for it in early)
                first_pat = False
            for head in range(2):
                for j in range(4):
                    pending_fin.append((hp, head, j, accs))

        pending_fin = []

        def finalize_piece(hp, head, j, accs):
            acc, an = accs[head]
            hr = slice(0, 64) if head == 0 else slice(64, 128)
            half = j // 2
            rb = psO[half * 64:(half + 1) * 64, (j % 2) * 512:(j % 2 + 1) * 512]
            rk = ("psO", j % 2)
            act(rb, acc[64:128, j * 512:(j + 1) * 512], AF.Ln, [(an, j)], [rk])
            act(rb, rb, AF.Exp, [rk], [rk], scale=-1.0)
            tt("dve", acc[hr, j * 512:(j + 1) * 512], acc[0:64, j * 512:(j + 1) * 512], rb, ALU.mult, [(an, j), rk], [(an, j)])
            tt("pool", mixT[hr, hp, j * 512:(j + 1) * 512], acc[hr, j * 512:(j + 1) * 512], gT[hr, j * 512:(j + 1) * 512], ALU.mult,
               [(an, j), ("gT", j)], [("mix", hp, j, head)])

        def flush_fin(n=None):
            k = len(pending_fin) if n is None else min(n, len(pending_fin))
            for _ in range(k):
                finalize_piece(*pending_fin.pop(0))

        def conv_phase():
            ub = fw[:, 0:2560]
            chs = fw[:, 2560:3072]
            sg = fw[:, 3072:3584]
            y = fw[:, 3584:4096]
            for ct in range(2):
                cw = consts[:, 8 + 3 * ct: 11 + 3 * ct]
                alt = (ct == 1)
                for n in range(3, 8):
                    j0 = (n - 3) * 512
                    bank = next_psI()
                    inproj(0, n, bank, alt)
                    act(chs, pI(bank), AF.Copy, [pIk(bank)], [("fw", 5)])
                    bank = next_psI()
                    inproj(256, n, bank, alt)
                    tt("dve", ub[:, j0:j0 + 512], pI(bank), chs, ALU.mult, [pIk(bank), ("fw", 5)], [("fw", n - 3)])
                    flush_fin(2)
                for m in range(4):
                    n = m + 4
                    j0 = (m + 1) * 512
                    bank = next_psI()
                    inproj(384, n, bank, alt)
                    act(sg, pI(bank), AF.Silu, [pIk(bank)], [("fw", 6)])
                    ukeys = [("fw", m), ("fw", m + 1)]
                    act(y, ub[:, j0 - 2:j0 + 510], AF.Copy, ukeys + ["consts"], [("fw", 7)], scale=cw[:, 0:1])
                    stt("dve", y, ub[:, j0 - 1:j0 + 511], cw[:, 1:2], y, ALU.mult, ALU.add, ukeys + [("fw", 7), "consts"], [("fw", 7)])
                    stt("dve", y, ub[:, j0:j0 + 512], cw[:, 2:3], y, ALU.mult, ALU.add, ukeys + [("fw", 7), "consts"], [("fw", 7)])
                    bank = next_psI()
                    inproj(128, n, bank, alt)
                    tt("dve", y, pI(bank), y, ALU.mult, [pIk(bank), ("fw", 7)], [("fw", 7)])
                    tt("pool", mixT[:, 6 + ct, m * 512:(m + 1) * 512], y, sg, ALU.mult, [("fw", 6), ("fw", 7)],
                       [("mix", 6 + ct, m, 0), ("mix", 6 + ct, m, 1)])

        def unit(u):
            if u == 0:
                wload(0)
            for i in range(3):
                P.dma("pool", "ldm%d" % i, masks[:, i, :], mk_d[u, i], writes=[("masks", i)])

            PW = 256
            xsb = [tab[:, 0:2048].rearrange("p (c t) -> p c t", c=8), tab[:, 2048:4096].rearrange("p (c t) -> p c t", c=8),
                   fw[:, 0:2048].rearrange("p (c t) -> p c t", c=8), fw[:, 2048:4096].rearrange("p (c t) -> p c t", c=8)]
            xkn = [("tab", 0), ("tab", 4), ("fw", 0), ("fw", 4)]
            vflat = Vp[:].rearrange("p n f -> p (n f)")
            sqb = [vflat[:, i * 2048:(i + 1) * 2048].rearrange("p (c t) -> p c t", c=8) for i in range(4)]
            sqk = [[("Vp", i * 8 + j) for j in range(8)] for i in range(4)]
            xsrc = xT_d[u].rearrange("(c p) t -> p c t", p=128)
            vpkeys = [("Vp", i) for i in range(32)]
            NPC = UT // PW

            def p1_bufs(n):
                i = n % 4
                kn, k0 = xkn[i]
                xk = [(kn, k0 + j) for j in range(4)]
                j2 = n % 2
                return (xsb[i], xk, sqb[i], sqk[i], fw2[:, j2 * 1024:j2 * 1024 + PW], fw2[:, j2 * 1024 + 512:j2 * 1024 + 512 + PW],
                        [("fw2", 2 * j2)], [("fw2", 2 * j2 + 1)])

            def p1_A1(n):
                xs, xk, sq, sk, rsb, rsc, krs, krc = p1_bufs(n)
                for q in range(2):
                    P.dma("sp", "ldx%d%d" % (n % 4 // 2, 2 * (n % 2) + q), xs[:, 4 * q:4 * q + 4, :], xsrc[:, 4 * q:4 * q + 4, n * PW:(n + 1) * PW],
                          writes=xk[2 * q:2 * q + 2])
                act(sq, xs, AF.Square, xk, sk)
                for c in range(5, 8):
                    act(xs[:, c, :], xs[:, c, :], AF.Copy, [xk[c // 2], "consts"], [xk[c // 2]], scale=gpre[:, c:c + 1])
                bank = next_psI()
                for c in range(8):
                    mm(pI(bank)[:, 0:PW], ones[:], sq[:, c, :], c == 0, c == 7, sk + ["ones"], [pIk(bank)], signal=(c == 7))
                act(rsc, pI(bank)[:, 0:PW], AF.Sqrt, [pIk(bank)], krc, bias=EPS, scale=1.0 / D)

            def p1_A2(n):
                xs, xk, sq, sk, rsb, rsc, krs, krc = p1_bufs(n)
                recip(rsb, rsc, krc, krs)

            def p1_B(n):
                xs, xk, sq, sk, rsb, rsc, krs, krc = p1_bufs(n)
                hk = [("hT", (n * PW) // 512)]
                for c in range(8):
                    dst = hT[:, c, n * PW:(n + 1) * PW]
                    if c < 5:
                        stt("dve", dst, xs[:, c, :], gpre[:, c:c + 1], rsb, ALU.mult, ALU.mult, [xk[c // 2], "consts"] + krs, hk)
                    else:
                        tt("pool", dst, xs[:, c, :], rsb, ALU.mult, [xk[c // 2]] + krs, hk)

            p1_A1(0)
            p1_A2(0)
            for n in range(1, NPC):
                p1_A1(n)
                p1_B(n - 1)
                p1_A2(n)
            p1_B(NPC - 1)
            for q in range(4):
                P.dma("sp", "ldt%d" % q, tab[:, q * 1024:(q + 1) * 1024], tab_d[u, :, q * 1024:(q + 1) * 1024],
                      writes=[("tab", 2 * q), ("tab", 2 * q + 1)])
            if u == 1:
                dump("hT", hT[:], [128, 8, UT], BF16, [("hT", n) for n in range(8)])
            mset("pool", Vp4[:, :, 1, :], 1.0, vpkeys)
            mset("pool", Vp4[:, :, 3, :], 1.0, vpkeys)

            for hp in range(6):
                for n in range(8):
                    bank = next_psI()
                    inproj(0, n, bank)
                    rope(bank, n, kT[:, n * 512:(n + 1) * 512], [("kT", n)])
                    bank = next_psI()
                    inproj(128, n, bank)
                    act(vT[:, n * 512:(n + 1) * 512], pI(bank), AF.Copy, [pIk(bank)], [("vT", n)])
                    if n < 4:
                        flush_fin(2)
                    if n >= 4:
                        m = n - 4
                        bank = next_psI()
                        inproj(256, n, bank)
                        rope(bank, n, qrot[:], ["qrot"])
                        tt("dve", qA[:, m * 512:(m + 1) * 512], qrot[:], mfull[:, 0, :], ALU.mult, ["qrot", "mfull"], [("qA", m)])
                        tt("pool", qB[:, m * 512:(m + 1) * 512], qrot[:], mfull[:, 1, :], ALU.mult, ["qrot", "mfull"], [("qB", m)])
                        for hd, (qsrc, qn) in enumerate(((qA, "qA"), (qB, "qB"))):
                            cpy("pool", q16[hd].rearrange("p (r m) -> p r m", m=128)[:, :, 32 * m:32 * m + 32],
                                qsrc[:, m * 512:(m + 1) * 512].rearrange("p (m r) -> p r m", r=16),
                                [(qn, m)], [("q16", hd, j) for j in range(4)])
                        bank = next_psI()
                        inproj(384, n, bank)
                        act(gT[:, m * 512:(m + 1) * 512], pI(bank), AF.Silu, [pIk(bank)], [("gT", m)])
                if u == 1 and hp == 0:
                    dump("kT", kT[:], [128, UT], BF16, [("kT", n) for n in range(8)])
                    dump("vT", vT[:], [128, UT], BF16, [("vT", n) for n in range(8)])
                    dump("qA", qA[:], [128, OWN], BF16, [("qA", n) for n in range(4)])
                    dump("qB", qB[:], [128, OWN], BF16, [("qB", n) for n in range(4)])
                    dump("gT", gT[:], [128, OWN], BF16, [("gT", n) for n in range(4)])
                wload(hp + 1)
                if hp == 5:
                    P.dma("pool", "ldo", hT[:, :, 0:1024], wo_d.rearrange("(c p) f -> p c f", p=128), writes=[("hT", 0), ("hT", 1)])
                attention(hp, u)

            P.dma("pool", "ldo2", wS2, w_d[:, 7 * 512:8 * 512].rearrange("(c p) f -> p c f", p=128), writes=W2KEYS)
            flush_fin(4)
            conv_phase()
            flush_fin()

            if u == 1:
                dump("mixT", mixT[:], [128, 8, OWN], BF16, [("mix", c, j, hd) for c in range(8) for j in range(4) for hd in range(2)])
            wout = hT[:, :, 0:1024]
            if u == 0:
                wload(0)
            gpb = tab[:, 0:1024]
            P.dma("sp", "ldg", gpb, gp_d, writes=[("tab", 0), ("tab", 1)])
            junk = fw2[:, 0:1024]
            for t in range(16):
                i = t % 2
                pb, pbn = ((psS, "psS"), (psO, "psO"), (psI, "psI"))[t % 3]
                xres = fw[:, i * 1024:(i + 1) * 1024]
                ob = fw[:, 2048 + i * 1024:2048 + (i + 1) * 1024]
                kx = [("fw", 2 * i), ("fw", 2 * i + 1)]
                ko = [("fw", 4 + 2 * i), ("fw", 5 + 2 * i)]
                ss = small[:, i:i + 1]
                kss = ("ss", i)
                if t == 0:
                    P.dma("sp", "ldr0", xres, xo_d[u, 0:128, :], writes=kx)
                if t + 1 < 16:
                    i1 = (t + 1) % 2
                    P.dma("sp", "ldr%d" % i1, fw[:, i1 * 1024:(i1 + 1) * 1024], xo_d[u, (t + 1) * 128:(t + 2) * 128, :],
                          writes=[("fw", 2 * i1), ("fw", 2 * i1 + 1)])
                for half in range(2):
                    for c in range(8):
                        mm(pb[:, half * 512:(half + 1) * 512], mixT[:, c, t * 128:(t + 1) * 128], wout[:, c, half * 512:(half + 1) * 512],
                           c == 0, c == 7, [("mix", c, t // 4, 0), ("mix", c, t // 4, 1), ("hT", 0), ("hT", 1)], [(pbn, half)], signal=(c == 7))
                act(junk, pb[:, :], AF.Square, [(pbn, 0), (pbn, 1)], [("fw2", 0), ("fw2", 1)])
                P.op("dve", lambda h, ss=ss: h.reduce_sum(ss, junk, mybir.AxisListType.X), [("fw2", 0), ("fw2", 1)], [kss])
                act(ss, ss, AF.Sqrt, [kss], [kss], bias=EPS, scale=1.0 / D)
                recip(ss, ss, [kss], [kss])
                stt("dve", ob, pb[:, :], ss, gpb, ALU.mult, ALU.mult, [(pbn, 0), (pbn, 1), kss, ("tab", 0), ("tab", 1)], ko)
                tt("pool", ob, ob, xres, ALU.add, kx + ko, ko)
                P.dma("sp", "st%d" % i, out_d[u, t * 128:(t + 1) * 128, :], ob, reads=ko)

        for u in range(2):
            unit(u)
        for e in ENGS:
            P.wait_all_dma(e, ["st0", "st1"])
        P.run(block)
    return nc


def _col_perm():
    perm = []
    for hp in range(6):
        hA, hB = 2 * hp, 2 * hp + 1
        rot = ([hA * 64 + i for i in range(32)] + [hB * 64 + i for i in range(32)]
               + [hA * 64 + 32 + i for i in range(32)] + [hB * 64 + 32 + i for i in range(32)])
        perm += [768 + c for c in rot]
        perm += [1536 + hp * 128 + i for i in range(128)]
        perm += rot
        perm += [2304 + hp * 128 + i for i in range(128)]
    for ct in range(2):
        for k4 in range(4):
            perm += [3072 + k4 * 256 + ct * 128 + i for i in range(128)]
    return np.asarray(perm)


def _tables(sb):
    posi = np.arange(UT, dtype=np.int64) + sb * OWN - OWN
    try:
        import jax
        import jax.numpy as jnp
        with jax.default_device(jax.devices("cpu")[0]):
            inv_freq = 10000.0 ** (-jnp.arange(32, dtype=jnp.float32) * 2.0 / 64)
            ang = jnp.asarray(posi.astype(np.int32)).astype(jnp.float32)[:, None] * inv_freq[None, :]
            c = np.asarray(jnp.cos(ang), dtype=np.float32).T
            s = np.asarray(jnp.sin(ang), dtype=np.float32).T
    except Exception:
        inv_freq = (np.float32(10000.0) ** (-np.arange(32, dtype=np.float32) * np.float32(2.0) / np.float32(64))).astype(np.float32)
        ang = (posi.astype(np.float32)[None, :] * inv_freq[:, None]).astype(np.float32)
        c = np.cos(ang).astype(np.float32)
        s = np.sin(ang).astype(np.float32)
    return np.ascontiguousarray(np.concatenate([c, c, s, s], axis=0))


def _masks(sb):
    p = np.arange(128)[:, None]
    i = np.arange(128)[None, :]
    M1 = np.where(p >= i, 1.0, 0.0).astype(np.float32)
    M2 = np.where(p <= i, 1.0, 0.0).astype(np.float32)
    MH = M1 if sb > 0 else np.zeros((128, 128), np.float32)
    MA = np.concatenate([M1, M2, M1, M2], axis=1)
    MB = np.concatenate([MH, M2, M1, M2], axis=1)
    MC = np.concatenate([MH, M2, MH, M2], axis=1)
    return np.stack([MA, MB, MC], axis=0)


_NC_CACHE = {}


def kernel(x, norm_pre_g, w_in, conv_w, w_out, norm_post_g):
    x = np.ascontiguousarray(np.asarray(x, dtype=np.float32))
    w_in = np.asarray(w_in, dtype=np.float32)
    wr = np.ascontiguousarray(w_in[:, _col_perm()])
    wout = np.ascontiguousarray(np.asarray(w_out, dtype=np.float32))
    gpost = np.ascontiguousarray(np.tile(np.asarray(norm_post_g, np.float32)[None, :], (128, 1)))
    consts = np.zeros((128, 16), np.float32)
    consts[:, 0:8] = np.asarray(norm_pre_g, np.float32).reshape(8, 128).T
    cw = np.asarray(conv_w, np.float32)
    for ct in range(2):
        consts[:, 8 + 3 * ct: 11 + 3 * ct] = cw[:, ct * 128:(ct + 1) * 128].T
    pp = np.arange(128)
    consts[:, 14] = ((pp // 32) % 2 == 0).astype(np.float32)
    consts[:, 15] = ((pp // 32) % 2 == 1).astype(np.float32)

    in_maps = []
    for c in range(NCORES):
        xT = np.zeros((2, D, UT), np.float32)
        xo = np.zeros((2, OWN, D), np.float32)
        tabs = np.zeros((2, 128, UT), np.float32)
        mk = np.zeros((2, 3, 128, 512), np.float32)
        for uu in range(2):
            g = 2 * c + uu
            b, sbk = g // 4, g % 4
            own = x[b, sbk * OWN:(sbk + 1) * OWN]
            xo[uu] = own
            xT[uu, :, OWN:] = own.T
            if sbk > 0:
                xT[uu, :, :OWN] = x[b, (sbk - 1) * OWN: sbk * OWN].T
            tabs[uu] = _tables(sbk)
            mk[uu] = _masks(sbk)
        in_maps.append({"xT": xT, "xown": xo, "tabs": tabs, "wr": wr, "wout": wout, "gpost": gpost,
                        "consts": consts, "masks": mk})
    if "nc" not in _NC_CACHE:
        _NC_CACHE["nc"] = build_nc()
    nc = _NC_CACHE["nc"]
    res = run_bass_kernel_spmd(nc, in_maps, core_ids=list(range(NCORES)))
    out = np.zeros((BATCH, SEQ, D), np.float32)
    for c in range(NCORES):
        o = np.asarray(res.results[c]["out"], dtype=np.float32)
        for uu in range(2):
            g = 2 * c + uu
            b, sbk = g // 4, g % 4
            out[b, sbk * OWN:(sbk + 1) * OWN] = o[uu]
    return out
```
